# Optimizing a Trainium2 kernel written in Bass

```python
import math
import jax, jax.numpy as jnp
from jax import lax
import numpy as np

D_MODEL = 4096
BATCH = 2
SEQ = 8192
DEPTH = 2

D_FF = 11008
FFN_RESIDUAL = 0.5
HEAD_DIM = 128
HEADS_A = 24
A_BRANCHES = ((128, 1), (512, 4), (2048, 16))
FNET_GROUPS = 8
FNET_GROUP_DIM = 128
QKV_DIM = HEADS_A * HEAD_DIM
FNET_DIM = FNET_GROUPS * FNET_GROUP_DIM
IN_AB = 3 * QKV_DIM + FNET_DIM
MIX_AB = QKV_DIM + FNET_DIM
REL_BUCKETS = 32
REL_MAX_DISTANCE = 1024
CONV_C_DIM = 2048
CONV_C_WIDTH = 31
CONV_D_DIM = 2048
CONV_D_WIDTH = 3
IN_CD = 2 * CONV_C_DIM + 3 * CONV_D_DIM
MIX_CD = CONV_C_DIM + CONV_D_DIM
N_EVEN = (DEPTH + 1) // 2
N_ODD = DEPTH // 2
NORM_EPS = 1e-6
NEG_INF = -1e30

kernel_name = "hybrid_dilated_fourier_conv_encoder"


def rms_norm(x, g):
    xf = x.astype(jnp.float32)
    y = xf * lax.rsqrt(jnp.mean(xf * xf, axis=-1, keepdims=True) + NORM_EPS)
    return (y * g.astype(jnp.float32)).astype(x.dtype)


def layer_norm(x, g, b):
    xf = x.astype(jnp.float32)
    mu = jnp.mean(xf, axis=-1, keepdims=True)
    xc = xf - mu
    y = xc * lax.rsqrt(jnp.mean(xc * xc, axis=-1, keepdims=True) + NORM_EPS)
    return (y * g.astype(jnp.float32) + b.astype(jnp.float32)).astype(x.dtype)


def swiglu(h, w_gate, w_up, w_down):
    return (jax.nn.silu(h @ w_gate) * (h @ w_up)) @ w_down


def depthwise_conv(x, w):
    width, ch = w.shape
    pad = (width - 1) // 2
    return lax.conv_general_dilated(
        x, w[:, None, :].astype(x.dtype), window_strides=(1,), padding=[(pad, pad)],
        dimension_numbers=("NWC", "WIO", "NWC"), feature_group_count=ch)


def t5_bucket(rel):
    half = REL_BUCKETS // 2
    max_exact = half // 2
    ret = jnp.where(rel > 0, half, 0)
    n = jnp.abs(rel)
    nf = jnp.maximum(n, 1).astype(jnp.float32)
    large = max_exact + (jnp.log(nf / max_exact) / math.log(REL_MAX_DISTANCE / max_exact)
                         * (half - max_exact)).astype(jnp.int32)
    large = jnp.minimum(large, half - 1)
    return ret + jnp.where(n < max_exact, n, large)


def dilated_branch(q, k, v, rel_bias, window, dilation):
    B, S, H, E = q.shape
    d = dilation
    K = window // (2 * dilation)
    L = S // d
    nb = -(-L // K)
    Lp = nb * K

    def strided(t):
        return t.reshape(B, L, d, H, E)

    qb = jnp.pad(strided(q), ((0, 0), (0, Lp - L), (0, 0), (0, 0), (0, 0))).reshape(B, nb, K, d, H, E)

    def key_blocks(t):
        tp = jnp.pad(strided(t), ((0, 0), (K, Lp - L + K), (0, 0), (0, 0), (0, 0)))
        return jnp.concatenate(
            [tp[:, j * K: j * K + Lp].reshape(B, nb, K, d, H, E) for j in range(3)], axis=2)

    kb = key_blocks(k)
    vb = key_blocks(v)
    a_idx = jnp.arange(K)[:, None]
    c_idx = jnp.arange(3 * K)[None, :]
    rel = c_idx - K - a_idx
    band = jnp.abs(rel) <= K
    n_key = jnp.arange(nb)[:, None] * K + jnp.arange(3 * K)[None, :] - K
    key_ok = (n_key >= 0) & (n_key < L)
    mask = band[None, :, :] & key_ok[:, None, :]
    bias = jnp.transpose(rel_bias[t5_bucket(rel * d)], (2, 0, 1)).astype(jnp.float32)
    scale = 1.0 / math.sqrt(E)
    s = jnp.einsum("bnqrhe,bnkrhe->bnrhqk", qb, kb, preferred_element_type=jnp.float32) * scale + bias
    s = jnp.where(mask[None, :, None, None], s, NEG_INF)
    lse = jax.nn.logsumexp(s, axis=-1)
    p = jnp.exp(s - lse[..., None])
    o = jnp.einsum("bnrhqk,bnkrhe->bnqrhe", p.astype(v.dtype), vb)
    o = o.reshape(B, Lp, d, H, E)[:, :L].reshape(B, S, H, E)
    lse = jnp.transpose(lse, (0, 1, 4, 2, 3)).reshape(B, Lp, d, H)[:, :L].reshape(B, S, H)
    return o, lse


def dilated_mixture(q, k, v, rel_bias):
    outs, lses = [], []
    for window, dilation in A_BRANCHES:
        o, l = dilated_branch(q, k, v, rel_bias, window, dilation)
        outs.append(o)
        lses.append(l)
    wts = jax.nn.softmax(jnp.stack(lses, axis=0), axis=0)
    return jnp.einsum("gbsh,gbshe->bshe", wts.astype(q.dtype), jnp.stack(outs, axis=0))


def fourier_mix(u):
    B, S, _ = u.shape
    ug = u.reshape(B, S, FNET_GROUPS, FNET_GROUP_DIM).astype(jnp.float32)
    f = jnp.fft.fft2(ug, axes=(1, 3), norm="ortho").real
    return f.reshape(B, S, FNET_DIM).astype(u.dtype)


def mixer_ab(h, w_in, w_out, rel_bias):
    B, S, _ = h.shape
    proj = h @ w_in
    q, k, v, u = jnp.split(proj, [QKV_DIM, 2 * QKV_DIM, 3 * QKV_DIM], axis=-1)
    q = q.reshape(B, S, HEADS_A, HEAD_DIM)
    k = k.reshape(B, S, HEADS_A, HEAD_DIM)
    v = v.reshape(B, S, HEADS_A, HEAD_DIM)
    a_out = dilated_mixture(q, k, v, rel_bias).reshape(B, S, QKV_DIM)
    b_out = fourier_mix(u)
    return jnp.concatenate([a_out, b_out], axis=-1) @ w_out


def mixer_cd(h, w_in, conv_c_w, conv_c_b, ln_c_g, ln_c_b, conv_d_w, w_out):
    proj = h @ w_in
    c_val, c_gate, d_b, d_c, d_h = jnp.split(
        proj, [CONV_C_DIM, 2 * CONV_C_DIM, 2 * CONV_C_DIM + CONV_D_DIM,
               2 * CONV_C_DIM + 2 * CONV_D_DIM], axis=-1)
    c = c_val * jax.nn.sigmoid(c_gate)
    c = depthwise_conv(c, conv_c_w) + conv_c_b
    c = jax.nn.silu(layer_norm(c, ln_c_g, ln_c_b))
    d = d_b * depthwise_conv(d_c * d_h, conv_d_w)
    return jnp.concatenate([c, d], axis=-1) @ w_out


def setup_inputs(seed: int = 0) -> dict:
    key = jax.random.key(seed)
    ks = jax.random.split(key, 24)
    f32 = jnp.float32

    def nrm(k, shape, scale):
        return jax.random.normal(k, shape, f32) * scale

    def gain(k, shape):
        return 1.0 + 0.02 * jax.random.normal(k, shape, f32)

    return {
        "x": jax.random.normal(ks[0], (BATCH, SEQ, D_MODEL), f32),
        "ffn1_norm": gain(ks[1], (DEPTH, D_MODEL)),
        "ffn1_w_gate": nrm(ks[2], (DEPTH, D_MODEL, D_FF), D_MODEL ** -0.5),
        "ffn1_w_up": nrm(ks[3], (DEPTH, D_MODEL, D_FF), D_MODEL ** -0.5),
        "ffn1_w_down": nrm(ks[4], (DEPTH, D_FF, D_MODEL), D_FF ** -0.5),
        "mix_norm": gain(ks[5], (DEPTH, D_MODEL)),
        "ffn2_norm": gain(ks[6], (DEPTH, D_MODEL)),
        "ffn2_w_gate": nrm(ks[7], (DEPTH, D_MODEL, D_FF), D_MODEL ** -0.5),
        "ffn2_w_up": nrm(ks[8], (DEPTH, D_MODEL, D_FF), D_MODEL ** -0.5),
        "ffn2_w_down": nrm(ks[9], (DEPTH, D_FF, D_MODEL), D_FF ** -0.5),
        "rel_bias": nrm(ks[10], (REL_BUCKETS, HEADS_A), 0.5),
        "w_in_ab": nrm(ks[11], (N_EVEN, D_MODEL, IN_AB), D_MODEL ** -0.5),
        "w_out_ab": nrm(ks[12], (N_EVEN, MIX_AB, D_MODEL), MIX_AB ** -0.5),
        "w_in_cd": nrm(ks[13], (N_ODD, D_MODEL, IN_CD), D_MODEL ** -0.5),
        "conv_c_w": nrm(ks[14], (N_ODD, CONV_C_WIDTH, CONV_C_DIM), CONV_C_WIDTH ** -0.5),
        "conv_c_b": nrm(ks[15], (N_ODD, CONV_C_DIM), 0.02),
        "ln_c_g": gain(ks[16], (N_ODD, CONV_C_DIM)),
        "ln_c_b": nrm(ks[17], (N_ODD, CONV_C_DIM), 0.02),
        "conv_d_w": nrm(ks[18], (N_ODD, CONV_D_WIDTH, CONV_D_DIM), CONV_D_WIDTH ** -0.5),
        "w_out_cd": nrm(ks[19], (N_ODD, MIX_CD, D_MODEL), MIX_CD ** -0.5),
        "final_norm": gain(ks[20], (D_MODEL,)),
    }


def reference(x, ffn1_norm, ffn1_w_gate, ffn1_w_up, ffn1_w_down, mix_norm,
              ffn2_norm, ffn2_w_gate, ffn2_w_up, ffn2_w_down, rel_bias,
              w_in_ab, w_out_ab, w_in_cd, conv_c_w, conv_c_b, ln_c_g, ln_c_b,
              conv_d_w, w_out_cd, final_norm):
    for layer in range(DEPTH):
        x = x + FFN_RESIDUAL * swiglu(rms_norm(x, ffn1_norm[layer]), ffn1_w_gate[layer],
                                      ffn1_w_up[layer], ffn1_w_down[layer])
        h = rms_norm(x, mix_norm[layer])
        if layer % 2 == 0:
            i = layer // 2
            x = x + mixer_ab(h, w_in_ab[i], w_out_ab[i], rel_bias)
        else:
            i = layer // 2
            x = x + mixer_cd(h, w_in_cd[i], conv_c_w[i], conv_c_b[i], ln_c_g[i], ln_c_b[i],
                             conv_d_w[i], w_out_cd[i])
        x = x + FFN_RESIDUAL * swiglu(rms_norm(x, ffn2_norm[layer]), ffn2_w_gate[layer],
                                      ffn2_w_up[layer], ffn2_w_down[layer])
    return rms_norm(x, final_norm)
```

```python
import math
import numpy as np
import ml_dtypes
import concourse.bass as bass
import concourse.mybir as mybir
from concourse.bass_utils import run_bass_kernel_spmd

F32 = mybir.dt.float32
BF16 = mybir.dt.bfloat16
AF = mybir.ActivationFunctionType
ALU = mybir.AluOpType
AX = mybir.AxisListType


class Cfg:
    def __init__(self, S=8192, D=4096, FF=11008, HEADS=24, FG=8, CC=2048, CD=2048, CW=31, DW=3):
        self.S, self.D, self.FF, self.HEADS, self.FG = S, D, FF, HEADS, FG
        self.CC, self.CD, self.CW, self.DW = CC, CD, CW, DW
        self.E = 128
        self.QKV = HEADS * 128
        self.FN = FG * 128
        self.IN_AB = 3 * self.QKV + self.FN
        self.MIX_AB = self.QKV + self.FN
        self.IN_CD = 2 * CC + 3 * CD
        self.MIX_CD = CC + CD
        self.T = 512
        self.NT = S // 512
        self.DC = D // 128
        self.FC = FF // 128
        self.BR = ((128, 1), (512, 4), (2048, 16))
        self.EPS = 1e-6


def lay_fm(w, cols=None):
    K, N = w.shape
    return np.ascontiguousarray(w.reshape(K // 128, 128, N // 128, 128).transpose(2, 1, 0, 3))


def lay_tm(w, G):
    K, N = w.shape
    return np.ascontiguousarray(
        w.reshape(K // (128 * G), G, 128, N // 512, 512).transpose(3, 0, 2, 1, 4))


def t5_bucket_np(rel):
    half, max_exact = 16, 8
    ret = np.where(rel > 0, half, 0)
    n = np.abs(rel)
    nf = np.maximum(n, 1).astype(np.float32)
    large = max_exact + (np.log(nf / max_exact) / math.log(1024 / max_exact)
                         * (half - max_exact)).astype(np.int32)
    large = np.minimum(large, half - 1)
    return ret + np.where(n < max_exact, n, large)


def bias_index_tables(cfg):
    k = np.arange(128)[:, None]
    j = np.arange(256)[None, :]
    rel = k + 64 - j
    band = np.abs(rel) <= 64
    idx = [t5_bucket_np(rel * d) for (_, d) in cfg.BR]
    return idx, band


def dft_tables(S):
    s = np.arange(S, dtype=np.int64)
    ph = (np.outer(s, s) % S).astype(np.float64) * (2.0 * np.pi / S)
    tab = np.empty((S, 2, S), dtype=ml_dtypes.bfloat16)
    tab[:, 0, :] = np.cos(ph).astype(np.float32).astype(ml_dtypes.bfloat16)
    tab[:, 1, :] = (-np.sin(ph)).astype(np.float32).astype(ml_dtypes.bfloat16)
    return tab


def chan_dft(n=128):
    c = np.arange(n, dtype=np.int64)
    ph = (np.outer(c, c) % n).astype(np.float64) * (2.0 * np.pi / n)
    return np.concatenate([np.cos(ph), np.sin(ph)], axis=1).astype(np.float32)


class K:
    def __init__(self, nc, stack):
        self.nc = nc
        self.stack = stack
        self.eng = {"pe": nc.tensor, "act": nc.scalar, "dve": nc.vector, "pool": nc.gpsimd, "sp": nc.sync}
        self.sem = {}
        self.cnt = {}
        self.seen = {e: {} for e in self.eng}
        for e in ("pe", "act", "dve", "pool"):
            self.sem[e] = stack.enter_context(nc.semaphore("sem_" + e))
            self.cnt[e] = 0
        self.nsem = 0
        self.free_sems = []
        self.live_sems = []

    def newsem(self, name):
        if self.free_sems:
            st = self.free_sems.pop()
        else:
            self.nsem += 1
            h = self.stack.enter_context(self.nc.semaphore(f"dsem_{self.nsem}"))
            self.sem[h] = h
            st = [h, 0]
        self.live_sems.append(st)
        return st

    def phase_end(self):
        self.free_sems.extend(self.live_sems)
        self.live_sems = []

    def sig(self, e, ins):
        ins.then_inc(self.sem[e], 1)
        self.cnt[e] += 1
        return (e, self.cnt[e])

    def wait(self, e, *tickets):
        for t in tickets:
            if t is None:
                continue
            src, val = t
            if self.seen[e].get(src, 0) >= val:
                continue
            semh = self.sem[src] if isinstance(src, str) else src
            self.eng[e].wait_ge(semh, val)
            self.seen[e][src] = val


def sl(start, n, step):
    return slice(start, start + (n - 1) * step + 1, step)


class DmaSlot:
    def __init__(self, k, name):
        self.k = k
        self.st = k.newsem(name)
        self.free = []

    def issue(self, e, dmas):
        k = self.k
        k.wait(e, *self.free)
        self.free = []
        for f in dmas:
            f(k.eng[e]).then_inc(self.st[0], 16)
            self.st[1] += 16
        return (self.st[0], self.st[1])


class PsumRing:
    def __init__(self, k, banks):
        self.k = k
        self.banks = banks
        self.free = [[] for _ in banks]
        self.i = 0

    def get(self):
        i = self.i
        self.i = (self.i + 1) % len(self.banks)
        fr = self.free[i]
        self.free[i] = []
        self.k.wait("pe", *fr)
        return i, self.banks[i]

    def release(self, i, *tickets):
        self.free[i] = list(tickets)


def build_program(cfg, debug_outs=()):
    from contextlib import ExitStack
    nc = bass.Bass("TRN2", target_bir_lowering=False)
    S, D, T, NT, DC, FC = cfg.S, cfg.D, cfg.T, cfg.NT, cfg.DC, cfg.FC
    GD = 2

    def din(name, shape, dt=F32):
        return nc.dram_tensor(name, list(shape), dt, kind="ExternalInput").ap()

    def dscr(name, shape, dt=F32):
        return nc.dram_tensor(name, list(shape), dt, kind="Internal").ap()

    x_in = din("x", [S, D])
    norms = din("norms", [7, D])
    normsT = din("normsT", [128, 7, DC])
    ffn_w = []
    for l in range(2):
        for f in range(2):
            ffn_w.append((din(f"wg{l}{f}", [FC, 128, DC, 128]), din(f"wu{l}{f}", [FC, 128, DC, 128]),
                          din(f"wd{l}{f}", [D // 512, FC // GD, 128, GD, 512])))
    NQK = (2 * cfg.QKV + cfg.FN) // 128
    w_ab_fm = din("w_ab_fm", [NQK, 128, DC, 128])
    w_ab_v = din("w_ab_v", [cfg.QKV // 512, DC // GD, 128, GD, 512])
    w_out_ab = din("w_out_ab", [D // 512, cfg.MIX_AB // 128 // GD, 128, GD, 512])
    w_cd_fm = din("w_cd_fm", [cfg.IN_CD // 128, 128, DC, 128])
    w_out_cd = din("w_out_cd", [D // 512, cfg.MIX_CD // 128 // GD, 128, GD, 512])
    biasT = din("biasT", [cfg.HEADS, 128, 3, 256])
    dft = din("dft", [S, 2, S], BF16)
    cdft = din("cdft", [128, 256])
    ident_in = din("identc", [128, 128])
    CG, DG = cfg.CC // 128, cfg.CD // 128
    convp = din("convp", [128, CG, cfg.CW + 3])
    convd = din("convd", [128, DG, cfg.DW])
    y_out = nc.dram_tensor("y", [S, D], F32, kind="ExternalOutput").ap()

    xa = dscr("xa", [S, D])
    xb = dscr("xb", [S, D])
    qT = dscr("qT", [cfg.QKV, S], BF16)
    kT = dscr("kT", [cfg.QKV, S], BF16)
    uT = dscr("uT", [cfg.FN, S], BF16)
    vv = dscr("vv", [S, cfg.QKV], BF16)
    mixT = dscr("mixT", [max(cfg.MIX_AB, cfg.MIX_CD), S], BF16)
    PADW = 16
    cpad = dscr("cpad", [cfg.CC, S + 2 * PADW])
    dpad = dscr("dpad", [cfg.CD, S + 2 * PADW])
    dbs = dscr("dbs", [cfg.CD, S])
    dbg = {}
    for nm, shape, dt in debug_outs:
        dbg[nm] = nc.dram_tensor("dbg_" + nm, list(shape), dt, kind="ExternalOutput").ap()

    with ExitStack() as stack:
        k = K(nc, stack)
        E = k.eng

        sbn = [0]

        def sb(name, shape, dt=F32, st=stack):
            sbn[0] += 1
            return st.enter_context(nc.sbuf_tensor(f"{name}_s{sbn[0]}", list(shape), dt))

        banks = [stack.enter_context(nc.psum_tensor(f"ps{i}", [128, 512], F32)) for i in range(8)]
        pr = PsumRing(k, banks)

        ident = sb("ident", [128, 128], BF16)
        identf = sb("identf", [128, 128], F32)
        ones_bf = sb("ones_bf", [128, 128], BF16)
        ones_f = sb("ones_f", [128, 128], F32)
        gcols = sb("gcols", [128, 7, DC], F32)
        cst = DmaSlot(k, "cst")
        E["pool"].memset(ones_f[:], 1.0)
        t_id0 = cst.issue("sp", [lambda e: e.dma_start(out=identf[:], in_=ident_in)])
        k.wait("dve", t_id0)
        t_id = k.sig("dve", E["dve"].tensor_copy(out=ident[:], in_=identf[:]))
        t_one = k.sig("pool", E["pool"].memset(ones_bf[:], 1.0))
        t_c0 = t_one
        t_g = cst.issue("sp", [lambda e: e.dma_start(out=gcols[:], in_=normsT)])
        const_ready = [t_c0, t_g, t_id]

        def dbg_sb(nm, ap):
            if nm in dbg:
                barrier()
                dsl = DmaSlot(k, "dbgs")
                k.wait("sp", dsl.issue("sp", [lambda e: e.dma_start(out=dbg[nm], in_=ap)]))

        def chain(e, ins):
            t = k.sig(e, ins)
            k.wait(e, t)
            return t

        def barrier():
            fin = [(e, k.cnt[e]) for e in ("pe", "act", "dve", "pool")]
            for e in ("pe", "act", "dve", "pool", "sp"):
                k.wait(e, *[t for t in fin if t[0] != e and t[1] > 0])
            k.free_sems.extend(st_ for st_ in k.live_sems if st_ is not cst.st)
            k.live_sems = [cst.st]

        def stepA(st, x_src, t0, norm_idx, hT, hT_free):
            xs, xs_slot, hb, ss, rstd = st["xs"], st["xs_slot"], st["hb"], st["ss"], st["rstd"]
            tickets = []
            for jj in range(4):
                r0 = t0 + jj * 128
                j = (t0 // T) * 4 + jj
                t_ld = xs_slot.issue("sp", [lambda e: e.dma_start(out=xs[:], in_=x_src[r0:r0 + 128, :])])
                k.wait("act", t_ld, st.get("hb_free"), st["ss_ready"])
                t_sq = k.sig("act", E["act"].activation(out=hb[:], in_=xs[:], func=AF.Square,
                                                        accum_out=ss[:, j:j + 1]))
                k.wait("dve", t_sq)
                t_r0 = k.sig("dve", E["dve"].tensor_scalar(out=rstd[:, j:j + 1], in0=ss[:, j:j + 1], scalar1=1.0 / D,
                                                           scalar2=cfg.EPS, op0=ALU.mult, op1=ALU.add))
                k.wait("act", t_r0)
                t_r1 = k.sig("act", E["act"].activation(out=rstd[:, j:j + 1], in_=rstd[:, j:j + 1], func=AF.Sqrt))
                k.wait("dve", t_r1)
                t_r2 = k.sig("dve", E["dve"].reciprocal(out=rstd[:, j:j + 1], in_=rstd[:, j:j + 1]))
                k.wait("dve", t_r2)
                t_hb = k.sig("dve", E["dve"].tensor_scalar(out=hb[:], in0=xs[:], scalar1=rstd[:, j:j + 1],
                                                           scalar2=None, op0=ALU.mult))
                xs_slot.free = [t_hb]
                k.wait("pe", t_hb)
                last_pe = None
                for b0 in range(0, DC, 4):
                    nb = min(4, DC - b0)
                    bi, bank = pr.get()
                    for q in range(nb):
                        dc = b0 + q
                        ins = E["pe"].transpose(out=st["psbf"][bi][:, q * 128:(q + 1) * 128],
                                                in_=hb[:, dc * 128:(dc + 1) * 128], identity=ident[:])
                    t_pe = k.sig("pe", ins)
                    last_pe = t_pe
                    k.wait("act", t_pe, *(hT_free if (jj == 0 and b0 == 0) else []))
                    for q in range(nb):
                        dc = b0 + q
                        ins = E["act"].activation(out=hT[:, dc, jj * 128:(jj + 1) * 128],
                                                  in_=st["psbf"][bi][:, q * 128:(q + 1) * 128],
                                                  func=AF.Copy, scale=gcols[:, norm_idx, dc:dc + 1])
                    t_ev = k.sig("act", ins)
                    pr.release(bi, t_ev)
                    tickets = [t_ev]
                st["hb_free"] = last_pe
            return tickets

        def fm_pass(hT, nK, hT_ready, w_l, wslots, jobs, epilogue):
            nslots = len(wslots)
            pend = {}

            def prefetch(ji):
                if ji >= len(jobs):
                    return
                slot, wt = wslots[ji % nslots]
                chunks = jobs[ji]
                pend[ji] = slot.issue("pool", [
                    (lambda e, ci=ci, c=c: e.dma_start(out=wt[:, ci, :], in_=w_l[c].rearrange("p a b -> p (a b)")))
                    for ci, c in enumerate(chunks)])

            for ji in range(min(nslots - 1, len(jobs))):
                prefetch(ji)
            last = None
            for ji, chunks in enumerate(jobs):
                prefetch(ji + nslots - 1)
                slot, wt = wslots[ji % nslots]
                k.wait("pe", pend.pop(ji), *hT_ready)
                got = [pr.get() for _ in chunks]
                for dc in range(nK):
                    for ci in range(len(chunks)):
                        ins = E["pe"].matmul(got[ci][1][:], lhsT=wt[:, ci, dc * 128:(dc + 1) * 128],
                                             rhs=hT[:, dc, :], start=(dc == 0), stop=(dc == nK - 1))
                t_pe = k.sig("pe", ins)
                slot.free = [t_pe]
                last = epilogue(ji, [g[1] for g in got], [g[0] for g in got], t_pe)
            return last

        def tm_pass(aT, nK, aT_ready, w_l, wslots, nslices, epilogue):
            nG = nK // GD
            nslots = len(wslots)
            seq = [(ds, g) for ds in range(nslices) for g in range(nG)]
            pend = {}

            def prefetch(i):
                if i >= len(seq):
                    return
                ds, g = seq[i]
                slot, wt = wslots[i % nslots]
                pend[i] = slot.issue("pool", [lambda e: e.dma_start(out=wt[:], in_=w_l[ds, g])])

            for i in range(min(nslots - 1, len(seq))):
                prefetch(i)
            i = 0
            last = []
            for ds in range(nslices):
                got = [pr.get() for _ in range(4)]
                for g in range(nG):
                    prefetch(i + nslots - 1)
                    slot, wt = wslots[i % nslots]
                    k.wait("pe", pend.pop(i), *aT_ready)
                    for gg in range(GD):
                        kc = g * GD + gg
                        for j in range(4):
                            ins = E["pe"].matmul(got[j][1][:], lhsT=aT[:, kc, j * 128:(j + 1) * 128],
                                                 rhs=wt[:, gg, :], start=(kc == 0), stop=(kc == nK - 1))
                    t_pe = k.sig("pe", ins)
                    slot.free = [t_pe]
                    i += 1
                last = []
                for j in range(4):
                    t = epilogue(ds, j, got[j][1], got[j][0], t_pe)
                    pr.release(got[j][0], t)
                    last.append(t)
            return last

        def ffn_phase(x_src, x_dst, norm_idx, wg_l, wu_l, wd_l):
            with ExitStack() as ps:
                hT = sb("hT", [128, DC, T], BF16, ps)
                actT = sb("actT", [128, FC, T], BF16, ps)
                st = {"xs": sb("xs", [128, D], F32, ps), "xs_slot": DmaSlot(k, "xs"),
                      "hb": sb("hb", [128, D], BF16, ps), "ss": sb("ss", [128, 4 * NT], F32, ps),
                      "rstd": sb("rstd", [128, 4 * NT], F32, ps)}
                st["psbf"] = [b[:].bitcast(BF16) for b in banks]
                gu = [(DmaSlot(k, "gu"), sb(f"gu{i}", [128, 2, DC * 128], BF16, ps)) for i in range(2)]
                wdl = [(DmaSlot(k, "wd"), sb(f"wdl{i}", [128, GD, 512], BF16, ps)) for i in range(4)]
                sg = [sb(f"sg{i}", [128, T], F32, ps) for i in range(2)]
                sg_free = [None, None]
                xr = [(DmaSlot(k, "xr"), sb(f"xr{i}", [128, 512], F32, ps)) for i in range(4)]
                ot = [(DmaSlot(k, "ot"), sb(f"ot{i}", [128, 512], F32, ps)) for i in range(4)]
                st["ss_ready"] = k.sig("pool", E["pool"].memset(st["ss"][:], 0.0))
                hT_free, actT_free = [], []
                cnt = [0]
                for ti in range(NT):
                    t0 = ti * T
                    hT_ready = stepA(st, x_src, t0, norm_idx, hT, hT_free)

                    def epi_b(ji, bks, bids, t_pe, actT_free_l=actT_free):
                        i2 = ji % 2
                        k.wait("act", t_pe, sg_free[i2])
                        t_a = k.sig("act", E["act"].activation(out=sg[i2][:], in_=bks[0][:], func=AF.Silu))
                        k.wait("dve", t_a, *(actT_free_l if ji == 0 else []))
                        t_d = k.sig("dve", E["dve"].tensor_tensor(out=actT[:, ji, :], in0=sg[i2][:],
                                                                  in1=bks[1][:], op=ALU.mult))
                        sg_free[i2] = t_d
                        pr.release(bids[0], t_d)
                        pr.release(bids[1], t_d)
                        return t_d

                    class WL:
                        def __getitem__(self, c):
                            return (wg_l if c[0] == 0 else wu_l)[c[1]]
                    t_act = fm_pass(hT, DC, hT_ready, WL(), gu, [[(0, fc), (1, fc)] for fc in range(FC)], epi_b)
                    hT_free = [(("pe", k.cnt["pe"]))]

                    def epi_c(ds, j, bank, bid, t_pe):
                        c = cnt[0]
                        cnt[0] += 1
                        xs_, xt = xr[c % 4]
                        os_, o_t = ot[c % 4]
                        r0 = t0 + j * 128
                        t_ld = xs_.issue("sp", [lambda e: e.dma_start(out=xt[:], in_=x_src[r0:r0 + 128, ds * 512:(ds + 1) * 512])])
                        k.wait("dve", t_pe, t_ld, *os_.free)
                        os_.free = []
                        t_d = k.sig("dve", E["dve"].scalar_tensor_tensor(out=o_t[:], in0=bank[:], scalar=0.5,
                                                                         in1=xt[:], op0=ALU.mult, op1=ALU.add))
                        xs_.free = [t_d]
                        k.wait("sp", t_d)
                        t_st = os_.issue("sp", [lambda e: e.dma_start(out=x_dst[r0:r0 + 128, ds * 512:(ds + 1) * 512], in_=o_t[:])])
                        os_.free = [t_st]
                        return t_d
                    tm_pass(actT, FC, [t_act], wd_l, wdl, D // 512, epi_c)
                    actT_free = [("pe", k.cnt["pe"])]
                dbg_sb("hT", hT[:])
                dbg_sb("actT", actT[:])
                for os_, _ in ot:
                    k.wait("sp", *os_.free)
                barrier()

        def proj_ab_phase(x_src, norm_idx):
            with ExitStack() as ps:
                hT = sb("hT", [128, DC, T], BF16, ps)
                st = {"xs": sb("xs", [128, D], F32, ps), "xs_slot": DmaSlot(k, "xs"),
                      "hb": sb("hb", [128, D], BF16, ps), "ss": sb("ss", [128, 4 * NT], F32, ps),
                      "rstd": sb("rstd", [128, 4 * NT], F32, ps)}
                st["psbf"] = [b[:].bitcast(BF16) for b in banks]
                ws = [(DmaSlot(k, "wfm"), sb(f"wfm{i}", [128, 1, DC * 128], BF16, ps)) for i in range(3)]
                wv = [(DmaSlot(k, "wv"), sb(f"wv{i}", [128, GD, 512], BF16, ps)) for i in range(6)]
                ob = [(DmaSlot(k, "ob"), sb(f"ob{i}", [128, 512], BF16, ps)) for i in range(4)]
                st["ss_ready"] = k.sig("pool", E["pool"].memset(st["ss"][:], 0.0))
                hT_free = []
                cnt = [0]
                nq = cfg.QKV // 128
                for ti in range(NT):
                    t0 = ti * T
                    hT_ready = stepA(st, x_src, t0, norm_idx, hT, hT_free)

                    def dst_fm(ji):
                        if ji < nq:
                            return qT[ji * 128:(ji + 1) * 128, t0:t0 + T]
                        if ji < 2 * nq:
                            return kT[(ji - nq) * 128:(ji - nq + 1) * 128, t0:t0 + T]
                        return uT[(ji - 2 * nq) * 128:(ji - 2 * nq + 1) * 128, t0:t0 + T]

                    def epi_fm(ji, bks, bids, t_pe):
                        c = cnt[0]
                        cnt[0] += 1
                        os_, o_t = ob[c % 4]
                        eng = "act" if c % 2 == 0 else "dve"
                        k.wait(eng, t_pe, *os_.free)
                        os_.free = []
                        if eng == "act":
                            t_d = k.sig("act", E["act"].copy(out=o_t[:], in_=bks[0][:]))
                        else:
                            t_d = k.sig("dve", E["dve"].tensor_copy(out=o_t[:], in_=bks[0][:]))
                        pr.release(bids[0], t_d)
                        k.wait("sp", t_d)
                        d = dst_fm(ji)
                        os_.free = [os_.issue("sp", [lambda e: e.dma_start(out=d, in_=o_t[:])])]
                        return t_d
                    fm_pass(hT, DC, hT_ready, w_ab_fm, ws, [[c] for c in range(NQK)], epi_fm)

                    def epi_v(ds, j, bank, bid, t_pe):
                        c = cnt[0]
                        cnt[0] += 1
                        os_, o_t = ob[c % 4]
                        eng = "act" if c % 2 == 0 else "dve"
                        k.wait(eng, t_pe, *os_.free)
                        os_.free = []
                        if eng == "act":
                            t_d = k.sig("act", E["act"].copy(out=o_t[:], in_=bank[:]))
                        else:
                            t_d = k.sig("dve", E["dve"].tensor_copy(out=o_t[:], in_=bank[:]))
                        k.wait("sp", t_d)
                        r0 = t0 + j * 128
                        os_.free = [os_.issue("sp", [lambda e: e.dma_start(out=vv[r0:r0 + 128, ds * 512:(ds + 1) * 512], in_=o_t[:])])]
                        return t_d
                    tm_pass(hT, DC, hT_ready, w_ab_v, wv, cfg.QKV // 512, epi_v)
                    hT_free = [("pe", k.cnt["pe"])]
                for os_, _ in ob:
                    k.wait("sp", *os_.free)
                barrier()

        def attn_phase():
            scale = 1.0 / math.sqrt(128.0)
            with ExitStack() as ps:
                kq = [(DmaSlot(k, "kq"), sb(f"kq{i}", [128, 2, S], BF16, ps)) for i in range(2)]
                eb = [(DmaSlot(k, "eb"), sb(f"eb{i}", [128, 3, 256], F32, ps)) for i in range(2)]
                vb = [(DmaSlot(k, "vb"), sb(f"vb{i}", [128, S // 128, 128], BF16, ps)) for i in range(2)]
                acc = sb("acc", [128, 2, S], F32, ps)
                ex = [sb(f"ex{i}", [128, 256], F32, ps) for i in range(3)]
                pT = [sb(f"pT{i}", [128, 256], BF16, ps) for i in range(3)]
                ex_free = [None] * 3
                pT_free = [None] * 3
                ob = [(DmaSlot(k, "ao"), sb(f"ao{i}", [128, 2048], BF16, ps)) for i in range(2)]
                rden = sb("rden", [128, 2048], F32, ps)
                acc_free = []
                vcount = 0
                cc = 0
                for h in range(cfg.HEADS):
                    ks, kqt = kq[h % 2]
                    t_kq = ks.issue("sp", [lambda e: e.dma_start(out=kqt[:, 0, :], in_=kT[h * 128:(h + 1) * 128, :]),
                                           lambda e: e.dma_start(out=kqt[:, 1, :], in_=qT[h * 128:(h + 1) * 128, :])])
                    es, ebt = eb[h % 2]
                    t_eb = es.issue("sp", [lambda e: e.dma_start(out=ebt[:], in_=biasT[h])])
                    k.wait("act", t_eb)
                    t_ebx = k.sig("act", E["act"].activation(out=ebt[:], in_=ebt[:], func=AF.Exp))
                    k.wait("pool", *acc_free)
                    t_z = k.sig("pool", E["pool"].memset(acc[:], 0.0))
                    last_dve = None
                    for gi, (win, d) in enumerate(cfg.BR):
                        L = S // d
                        nch = L // 128
                        vs, vt = vb[vcount % 2]
                        vcount += 1
                        vsrc = vv[:, h * 128:(h + 1) * 128].rearrange("(c p r) e -> r p c e", p=128, r=d)
                        t_v = vs.issue("sp", [(lambda e, r=r: e.dma_start(out=vt[:, r * nch:(r + 1) * nch, :], in_=vsrc[r]))
                                              for r in range(d)])
                        for r in range(d):
                            prev = None
                            for c in range(nch):
                                q0 = max(0, 128 * c - 64)
                                q1 = min(L, 128 * c + 192)
                                j0 = q0 - (128 * c - 64)
                                N = q1 - q0
                                bi, bank = pr.get()
                                k.wait("pe", t_kq)
                                ksl = kqt[:, 0, sl(r + d * 128 * c, 128, d)]
                                qsl = kqt[:, 1, sl(r + d * q0, N, d)]
                                t_s = k.sig("pe", E["pe"].matmul(bank[:, 0:N], lhsT=ksl, rhs=qsl, start=True, stop=True))
                                i3 = cc % 3
                                cc += 1
                                k.wait("act", t_s, ex_free[i3])
                                t_e = k.sig("act", E["act"].activation(out=ex[i3][:, 0:N], in_=bank[:, 0:N],
                                                                       func=AF.Exp, scale=scale))
                                pr.release(bi, t_e)
                                k.wait("dve", t_e, t_ebx, pT_free[i3])
                                t_p = k.sig("dve", E["dve"].tensor_tensor(out=pT[i3][:, 0:N], in0=ex[i3][:, 0:N],
                                                                          in1=ebt[:, gi, j0:j0 + N], op=ALU.mult))
                                ex_free[i3] = t_p
                                k.wait("pe", t_p, t_v)
                                vch = vt[:, r * nch + c, :]
                                nl = 128 * c + 64 - q0
                                if c > 0:
                                    pbi, pbank = prev
                                    E["pe"].matmul(pbank[:, 0:128], lhsT=vch, rhs=pT[i3][:, 0:nl], start=False, stop=True, skip_group_check=True)
                                    t_l = k.sig("pe", E["pe"].matmul(pbank[:, 128:256], lhsT=ones_bf[:], rhs=pT[i3][:, 0:nl],
                                                                      start=False, stop=True, skip_group_check=True))
                                    done_blocks = [(pbi, pbank, 128 * c - 64, 128, t_l)]
                                else:
                                    nbi, nbank = pr.get()
                                    E["pe"].matmul(nbank[:, 0:nl], lhsT=vch, rhs=pT[i3][:, 0:nl], start=True, stop=True, skip_group_check=True)
                                    t_l = k.sig("pe", E["pe"].matmul(nbank[:, 128:128 + nl], lhsT=ones_bf[:], rhs=pT[i3][:, 0:nl],
                                                                      start=False, stop=True, skip_group_check=True))
                                    done_blocks = [(nbi, nbank, 0, nl, t_l)]
                                nr = q1 - (128 * c + 64)
                                nbi, nbank = pr.get()
                                last_c = (c == nch - 1)
                                E["pe"].matmul(nbank[:, 0:nr], lhsT=vch, rhs=pT[i3][:, nl:nl + nr], start=True, stop=last_c, skip_group_check=True)
                                t_r = k.sig("pe", E["pe"].matmul(nbank[:, 128:128 + nr], lhsT=ones_bf[:], rhs=pT[i3][:, nl:nl + nr],
                                                                  start=False, stop=last_c, skip_group_check=True))
                                pT_free[i3] = t_r
                                prev = (nbi, nbank)
                                if last_c:
                                    done_blocks.append((nbi, nbank, 128 * c + 64, nr, t_r))
                                for (dbi, dbank, m0, n, t_done) in done_blocks:
                                    k.wait("dve", t_done, t_z)
                                    dst = acc[:, :, sl(r + d * m0, n, d)]
                                    src = dbank[:, 0:256].rearrange("p (a b) -> p a b", a=2)[:, :, 0:n]
                                    t_a = k.sig("dve", E["dve"].tensor_tensor(out=dst, in0=dst, in1=src, op=ALU.add))
                                    pr.release(dbi, t_a)
                                    last_dve = t_a
                        vs.free = [("pe", k.cnt["pe"])]
                    ks.free = [("pe", k.cnt["pe"])]
                    es.free = [last_dve]
                    for o0 in range(0, S, 2048):
                        os_, o_t = ob[(o0 // 2048) % 2]
                        k.wait("dve", last_dve, *os_.free)
                        os_.free = []
                        t_rd = k.sig("dve", E["dve"].reciprocal(out=rden[:], in_=acc[:, 1, o0:o0 + 2048]))
                        k.wait("dve", t_rd)
                        t_o = k.sig("dve", E["dve"].tensor_tensor(out=o_t[:], in0=acc[:, 0, o0:o0 + 2048], in1=rden[:], op=ALU.mult))
                        k.wait("sp", t_o)
                        os_.free = [os_.issue("sp", [lambda e: e.dma_start(out=mixT[h * 128:(h + 1) * 128, o0:o0 + 2048], in_=o_t[:])])]
                        acc_free = [t_o]
                for os_, _ in ob:
                    k.wait("sp", *os_.free)
                barrier()

        def fnet_phase():
            NS = S // 128
            GH = min(4, cfg.FG)
            with ExitStack() as ps:
                cs = sb("cs", [128, 256], F32, ps)
                csb = sb("csb", [128, 256], BF16, ps)
                ut = [(DmaSlot(k, "ut"), sb(f"ut{i}", [128, S], BF16, ps)) for i in range(2)]
                ab = sb("ab", [128, GH, NS, 256], BF16, ps)
                tb = [(DmaSlot(k, "tb"), sb(f"tb{i}", [128, 2, 512], BF16, ps)) for i in range(4)]
                ob = [(DmaSlot(k, "fo"), sb(f"fo{i}", [128, 512], BF16, ps)) for i in range(4)]
                t_cs = cst.issue("sp", [lambda e: e.dma_start(out=cs[:], in_=cdft)])
                k.wait("dve", t_cs)
                t_csb = k.sig("dve", E["dve"].tensor_copy(out=csb[:], in_=cs[:]))
                ab_free = []
                ocnt = 0
                tcnt = 0
                for g0 in range(0, cfg.FG, GH):
                    for gg in range(GH):
                        g = g0 + gg
                        us, u_t = ut[g % 2]
                        t_u = us.issue("sp", [lambda e: e.dma_start(out=u_t[:], in_=uT[g * 128:(g + 1) * 128, :])])
                        for sc in range(0, NS, 2):
                            bi, bank = pr.get()
                            k.wait("pe", t_u, t_csb)
                            for q in range(2):
                                ins = E["pe"].matmul(bank[:, q * 256:(q + 1) * 256], lhsT=u_t[:, (sc + q) * 128:(sc + q + 1) * 128],
                                                     rhs=csb[:], start=True, stop=True)
                            t_pe = k.sig("pe", ins)
                            eng = "act" if (sc // 2) % 2 == 0 else "dve"
                            k.wait(eng, t_pe, *(ab_free if (gg == 0 and sc == 0) else []))
                            dst = ab[:, gg, sc:sc + 2, :]
                            src = bank[:].rearrange("p (a b) -> p a b", a=2)
                            if eng == "act":
                                t_e = k.sig("act", E["act"].copy(out=dst, in_=src))
                            else:
                                t_e = k.sig("dve", E["dve"].tensor_copy(out=dst, in_=src))
                            pr.release(bi, t_e)
                        us.free = [("pe", k.cnt["pe"])]
                    ab_ready = [("act", k.cnt["act"]), ("dve", k.cnt["dve"])]
                    for st_ in range(S // 512):
                        got = [pr.get() for _ in range(GH)]
                        for sc in range(NS):
                            ts, t_t = tb[tcnt % 4]
                            tcnt += 1
                            t_ld = ts.issue("sp", [lambda e: e.dma_start(out=t_t[:], in_=dft[sc * 128:(sc + 1) * 128, :, st_ * 512:(st_ + 1) * 512])])
                            k.wait("pe", t_ld, *ab_ready)
                            for gg in range(GH):
                                for q in range(2):
                                    ins = E["pe"].matmul(got[gg][1][:], lhsT=ab[:, gg, sc, q * 128:(q + 1) * 128], rhs=t_t[:, q, :],
                                                         start=(sc == 0 and q == 0), stop=(sc == NS - 1 and q == 1))
                            t_pe = k.sig("pe", ins)
                            ts.free = [t_pe]
                        for gg in range(GH):
                            os_, o_t = ob[ocnt % 4]
                            ocnt += 1
                            k.wait("act", t_pe, *os_.free)
                            os_.free = []
                            t_e = k.sig("act", E["act"].activation(out=o_t[:], in_=got[gg][1][:], func=AF.Copy,
                                                                   scale=1.0 / math.sqrt(S * 128.0)))
                            pr.release(got[gg][0], t_e)
                            k.wait("sp", t_e)
                            row = cfg.QKV + (g0 + gg) * 128
                            os_.free = [os_.issue("sp", [lambda e: e.dma_start(out=mixT[row:row + 128, st_ * 512:(st_ + 1) * 512], in_=o_t[:])])]
                    ab_free = [("pe", k.cnt["pe"])]
                for os_, _ in ob:
                    k.wait("sp", *os_.free)
                barrier()

        def outproj_phase(x_src, x_dst, nKc, w_l):
            with ExitStack() as ps:
                mT = [(DmaSlot(k, "mT"), sb(f"mT{i}", [128, nKc, T], BF16, ps)) for i in range(2)]
                wdl = [(DmaSlot(k, "wo"), sb(f"wo{i}", [128, GD, 512], BF16, ps)) for i in range(6)]
                xr = [(DmaSlot(k, "xr"), sb(f"xr{i}", [128, 512], F32, ps)) for i in range(4)]
                ot = [(DmaSlot(k, "ot"), sb(f"ot{i}", [128, 512], F32, ps)) for i in range(4)]
                cnt = [0]
                for ti in range(NT):
                    t0 = ti * T
                    ms, m_t = mT[ti % 2]
                    t_m = ms.issue("sp", [lambda e: e.dma_start(out=m_t[:], in_=mixT[0:nKc * 128, t0:t0 + T].rearrange("(c p) t -> p c t", p=128))])

                    def epi_c(ds, j, bank, bid, t_pe):
                        c = cnt[0]
                        cnt[0] += 1
                        xs_, xt = xr[c % 4]
                        os_, o_t = ot[c % 4]
                        r0 = t0 + j * 128
                        t_ld = xs_.issue("sp", [lambda e: e.dma_start(out=xt[:], in_=x_src[r0:r0 + 128, ds * 512:(ds + 1) * 512])])
                        k.wait("dve", t_pe, t_ld, *os_.free)
                        os_.free = []
                        t_d = k.sig("dve", E["dve"].tensor_tensor(out=o_t[:], in0=bank[:], in1=xt[:], op=ALU.add))
                        xs_.free = [t_d]
                        k.wait("sp", t_d)
                        os_.free = [os_.issue("sp", [lambda e: e.dma_start(out=x_dst[r0:r0 + 128, ds * 512:(ds + 1) * 512], in_=o_t[:])])]
                        return t_d
                    tm_pass(m_t, nKc, [t_m], w_l, wdl, D // 512, epi_c)
                    ms.free = [("pe", k.cnt["pe"])]
                for os_, _ in ot:
                    k.wait("sp", *os_.free)
                barrier()

        def proj_cd_phase(x_src, norm_idx):
            with ExitStack() as ps:
                hT = sb("hT", [128, DC, T], BF16, ps)
                st = {"xs": sb("xs", [128, D], F32, ps), "xs_slot": DmaSlot(k, "xs"),
                      "hb": sb("hb", [128, D], BF16, ps), "ss": sb("ss", [128, 4 * NT], F32, ps),
                      "rstd": sb("rstd", [128, 4 * NT], F32, ps)}
                st["psbf"] = [b[:].bitcast(BF16) for b in banks]
                ws = [(DmaSlot(k, "wcd"), sb(f"wcd{i}", [128, 2, DC * 128], BF16, ps)) for i in range(2)]
                sg = [sb(f"sg{i}", [128, T], F32, ps) for i in range(2)]
                sg_free = [None, None]
                ob = [(DmaSlot(k, "oc"), sb(f"oc{i}", [128, 512], F32, ps)) for i in range(4)]
                zt = sb("zt", [128, PADW], F32, ps)
                st["ss_ready"] = k.sig("pool", E["pool"].memset(st["ss"][:], 0.0))
                t_z = k.sig("pool", E["pool"].memset(zt[:], 0.0))
                k.wait("sp", t_z)
                zs = DmaSlot(k, "zs")
                zd = []
                for buf, n in ((cpad, CG), (dpad, DG)):
                    for gch in range(n):
                        for off in (0, PADW + S):
                            zd.append(lambda e, buf=buf, gch=gch, off=off: e.dma_start(out=buf[gch * 128:(gch + 1) * 128, off:off + PADW], in_=zt[:]))
                t_zd = zs.issue("sp", zd)
                k.wait("sp", t_zd)
                jobs = []
                for i in range(CG):
                    jobs.append([i, CG + i])
                for i in range(DG):
                    jobs.append([2 * CG + DG + i, 2 * CG + 2 * DG + i])
                for i in range(DG):
                    jobs.append([2 * CG + i])
                hT_free = []
                cnt = [0]
                for ti in range(NT):
                    t0 = ti * T
                    hT_ready = stepA(st, x_src, t0, norm_idx, hT, hT_free)

                    def epi(ji, bks, bids, t_pe):
                        c = cnt[0]
                        cnt[0] += 1
                        os_, o_t = ob[c % 4]
                        if len(bks) == 2:
                            i2 = c % 2
                            func = AF.Sigmoid if ji < CG else AF.Copy
                            k.wait("act", t_pe, sg_free[i2])
                            t_a = k.sig("act", E["act"].activation(out=sg[i2][:], in_=bks[1][:], func=func))
                            k.wait("dve", t_a, *os_.free)
                            os_.free = []
                            t_d = k.sig("dve", E["dve"].tensor_tensor(out=o_t[:], in0=sg[i2][:], in1=bks[0][:], op=ALU.mult))
                            sg_free[i2] = t_d
                            pr.release(bids[0], t_d)
                            pr.release(bids[1], t_d)
                            if ji < CG:
                                d = cpad[ji * 128:(ji + 1) * 128, PADW + t0:PADW + t0 + T]
                            else:
                                d = dpad[(ji - CG) * 128:(ji - CG + 1) * 128, PADW + t0:PADW + t0 + T]
                        else:
                            k.wait("act", t_pe, *os_.free)
                            os_.free = []
                            t_d = k.sig("act", E["act"].copy(out=o_t[:], in_=bks[0][:]))
                            pr.release(bids[0], t_d)
                            i = ji - CG - DG
                            d = dbs[i * 128:(i + 1) * 128, t0:t0 + T]
                        k.wait("sp", t_d)
                        os_.free = [os_.issue("sp", [lambda e: e.dma_start(out=d, in_=o_t[:])])]
                        return t_d
                    fm_pass(hT, DC, hT_ready, w_cd_fm, ws, jobs, epi)
                    hT_free = [("pe", k.cnt["pe"])]
                for os_, _ in ob:
                    k.wait("sp", *os_.free)
                barrier()

        def conv_phase():
            W = cfg.CW
            padc = (W - 1) // 2
            TW = T + 2 * PADW
            with ExitStack() as ps:
                cp = sb("cp", [128, CG, W + 3], F32, ps)
                dp = sb("dp", [128, DG, cfg.DW], F32, ps)
                cin = [(DmaSlot(k, "cin"), sb(f"cin{i}", [128, TW], F32, ps)) for i in range(3)]
                co = sb("co", [128, CG, T], F32, ps)
                sq = [sb(f"sq{i}", [128, T], F32, ps) for i in range(2)]
                sq_free = [None, None]
                mean = sb("mean", [128, T], F32, ps)
                rstd = sb("rstdc", [128, T], F32, ps)
                tmp = [sb(f"tmpc{i}", [128, T], F32, ps) for i in range(2)]
                ob = [(DmaSlot(k, "co"), sb(f"cob{i}", [128, T], BF16, ps)) for i in range(4)]
                dbt = [(DmaSlot(k, "dbt"), sb(f"dbt{i}", [128, T], F32, ps)) for i in range(2)]
                t_p = cst.issue("sp", [lambda e: e.dma_start(out=cp[:], in_=convp), lambda e: e.dma_start(out=dp[:], in_=convd)])
                k.wait("dve", t_p)
                k.wait("act", t_p)
                co_free = []
                tmp_free = [None, None]
                ic = 0
                oc = 0
                for ti in range(NT):
                    t0 = ti * T
                    bs, bsum = pr.get()
                    bq, bsq = pr.get()
                    for g in range(CG):
                        cs_, c_t = cin[ic % 3]
                        ic += 1
                        t_ld = cs_.issue("sp", [lambda e: e.dma_start(out=c_t[:], in_=cpad[g * 128:(g + 1) * 128, t0:t0 + TW])])
                        k.wait("dve", t_ld, *(co_free if g == 0 else []))
                        o = co[:, g, :]
                        base = PADW - padc
                        t_cv = chain("dve", E["dve"].tensor_scalar(out=o, in0=c_t[:, base:base + T], scalar1=cp[:, g, 0:1], scalar2=cp[:, g, W:W + 1],
                                                                   op0=ALU.mult, op1=ALU.add))
                        for tap in range(1, W):
                            t_cv = chain("dve", E["dve"].scalar_tensor_tensor(out=o, in0=c_t[:, base + tap:base + tap + T], scalar=cp[:, g, tap:tap + 1],
                                                                              in1=o, op0=ALU.mult, op1=ALU.add))
                        cs_.free = [t_cv]
                        i2 = g % 2
                        k.wait("act", t_cv, sq_free[i2])
                        t_sq = k.sig("act", E["act"].activation(out=sq[i2][:], in_=o, func=AF.Square))
                        k.wait("pe", t_cv, t_sq)
                        E["pe"].matmul(bsum[:], lhsT=ones_f[:], rhs=o, start=(g == 0), stop=(g == CG - 1))
                        t_m = k.sig("pe", E["pe"].matmul(bsq[:], lhsT=ones_f[:], rhs=sq[i2][:], start=(g == 0), stop=(g == CG - 1)))
                        sq_free[i2] = t_m
                    k.wait("dve", t_m)
                    chain("dve", E["dve"].tensor_scalar(out=mean[:], in0=bsum[:], scalar1=1.0 / cfg.CC, scalar2=None, op0=ALU.mult))
                    chain("dve", E["dve"].tensor_scalar(out=rstd[:], in0=bsq[:], scalar1=1.0 / cfg.CC, scalar2=None, op0=ALU.mult))
                    t1 = chain("dve", E["dve"].tensor_tensor(out=tmp[0][:], in0=mean[:], in1=mean[:], op=ALU.mult))
                    pr.release(bs, t1)
                    pr.release(bq, t1)
                    chain("dve", E["dve"].tensor_tensor(out=rstd[:], in0=rstd[:], in1=tmp[0][:], op=ALU.subtract))
                    t3 = chain("dve", E["dve"].tensor_scalar(out=rstd[:], in0=rstd[:], scalar1=cfg.EPS, scalar2=None, op0=ALU.add))
                    k.wait("act", t3)
                    t3b = k.sig("act", E["act"].activation(out=rstd[:], in_=rstd[:], func=AF.Sqrt))
                    k.wait("dve", t3b)
                    chain("dve", E["dve"].reciprocal(out=rstd[:], in_=rstd[:]))
                    for g in range(CG):
                        i2 = g % 2
                        os_, o_t = ob[oc % 4]
                        oc += 1
                        k.wait("dve", tmp_free[i2])
                        chain("dve", E["dve"].tensor_tensor(out=tmp[i2][:], in0=co[:, g, :], in1=mean[:], op=ALU.subtract))
                        t4 = chain("dve", E["dve"].tensor_tensor(out=tmp[i2][:], in0=tmp[i2][:], in1=rstd[:], op=ALU.mult))
                        k.wait("act", t4, *os_.free)
                        os_.free = []
                        t5 = k.sig("act", E["act"].activation(out=o_t[:], in_=tmp[i2][:], func=AF.Silu, scale=cp[:, g, W + 1:W + 2],
                                                               bias=cp[:, g, W + 2:W + 3]))
                        tmp_free[i2] = t5
                        k.wait("sp", t5)
                        os_.free = [os_.issue("sp", [lambda e: e.dma_start(out=mixT[g * 128:(g + 1) * 128, t0:t0 + T], in_=o_t[:])])]
                    co_free = [("dve", k.cnt["dve"])]
                    for g in range(DG):
                        cs_, c_t = cin[ic % 3]
                        ic += 1
                        bs_, b_t = dbt[g % 2]
                        os_, o_t = ob[oc % 4]
                        oc += 1
                        t_ld = cs_.issue("sp", [lambda e: e.dma_start(out=c_t[:], in_=dpad[g * 128:(g + 1) * 128, t0:t0 + TW])])
                        t_lb = bs_.issue("sp", [lambda e: e.dma_start(out=b_t[:], in_=dbs[g * 128:(g + 1) * 128, t0:t0 + T])])
                        i2 = g % 2
                        pd = (cfg.DW - 1) // 2
                        base = PADW - pd
                        k.wait("dve", t_ld, t_lb, tmp_free[i2], *os_.free)
                        os_.free = []
                        t6 = chain("dve", E["dve"].tensor_scalar(out=tmp[i2][:], in0=c_t[:, base:base + T], scalar1=dp[:, g, 0:1], scalar2=None, op0=ALU.mult))
                        for tap in range(1, cfg.DW):
                            t6 = chain("dve", E["dve"].scalar_tensor_tensor(out=tmp[i2][:], in0=c_t[:, base + tap:base + tap + T], scalar=dp[:, g, tap:tap + 1],
                                                                            in1=tmp[i2][:], op0=ALU.mult, op1=ALU.add))
                        t7 = k.sig("dve", E["dve"].tensor_tensor(out=o_t[:], in0=tmp[i2][:], in1=b_t[:], op=ALU.mult))
                        cs_.free = [t7]
                        bs_.free = [t7]
                        k.wait("sp", t7)
                        row = cfg.CC + g * 128
                        os_.free = [os_.issue("sp", [lambda e: e.dma_start(out=mixT[row:row + 128, t0:t0 + T], in_=o_t[:])])]
                for os_, _ in ob:
                    k.wait("sp", *os_.free)
                barrier()

        def final_norm_phase(x_src, norm_idx):
            with ExitStack() as ps:
                grep = sb("grep", [128, D], F32, ps)
                xs = [(DmaSlot(k, "fx"), sb(f"fx{i}", [128, D], F32, ps)) for i in range(2)]
                ys = [(DmaSlot(k, "fy"), sb(f"fy{i}", [128, D], F32, ps)) for i in range(2)]
                junk = sb("junk", [128, D], BF16, ps)
                ss = sb("fss", [128, S // 128], F32, ps)
                rs = sb("frs", [128, S // 128], F32, ps)
                t_g = cst.issue("sp", [lambda e: e.dma_start(out=grep[:], in_=norms[norm_idx:norm_idx + 1, :].partition_broadcast(128))])
                t_z = k.sig("pool", E["pool"].memset(ss[:], 0.0))
                for i in range(S // 128):
                    xs_, xt = xs[i % 2]
                    ys_, yt = ys[i % 2]
                    t_ld = xs_.issue("sp", [lambda e: e.dma_start(out=xt[:], in_=x_src[i * 128:(i + 1) * 128, :])])
                    k.wait("act", t_ld, t_z)
                    t_sq = k.sig("act", E["act"].activation(out=junk[:], in_=xt[:], func=AF.Square, accum_out=ss[:, i:i + 1]))
                    k.wait("dve", t_sq, t_g, *ys_.free)
                    ys_.free = []
                    t_r = k.sig("dve", E["dve"].tensor_scalar(out=rs[:, i:i + 1], in0=ss[:, i:i + 1], scalar1=1.0 / D, scalar2=cfg.EPS, op0=ALU.mult, op1=ALU.add))
                    k.wait("act", t_r)
                    t_r1 = k.sig("act", E["act"].activation(out=rs[:, i:i + 1], in_=rs[:, i:i + 1], func=AF.Sqrt))
                    k.wait("dve", t_r1)
                    t_r2 = k.sig("dve", E["dve"].reciprocal(out=rs[:, i:i + 1], in_=rs[:, i:i + 1]))
                    k.wait("dve", t_r2)
                    t_y = k.sig("dve", E["dve"].scalar_tensor_tensor(out=yt[:], in0=xt[:], scalar=rs[:, i:i + 1], in1=grep[:],
                                                                     op0=ALU.mult, op1=ALU.mult))
                    xs_.free = [t_y]
                    k.wait("sp", t_y)
                    ys_.free = [ys_.issue("sp", [lambda e: e.dma_start(out=y_out[i * 128:(i + 1) * 128, :], in_=yt[:])])]
                for ys_, _ in ys:
                    k.wait("sp", *ys_.free)
                barrier()

        for e in ("pe", "act", "dve", "pool", "sp"):
            k.wait(e, *const_ready)
        phases = build_program.phases
        if "ffn00" in phases:
            ffn_phase(x_in, xa, 0, *ffn_w[0])
        if "projab" in phases:
            proj_ab_phase(xa, 1)
        if "attn" in phases:
            attn_phase()
        if "fnet" in phases:
            fnet_phase()
        if "outab" in phases:
            outproj_phase(xa, xb, cfg.MIX_AB // 128, w_out_ab)
        if "ffn01" in phases:
            ffn_phase(xb, xa, 2, *ffn_w[1])
        if "ffn10" in phases:
            ffn_phase(xa, xb, 3, *ffn_w[2])
        if "projcd" in phases:
            proj_cd_phase(xb, 4)
        if "conv" in phases:
            conv_phase()
        if "outcd" in phases:
            outproj_phase(xb, xa, cfg.MIX_CD // 128, w_out_cd)
        if "ffn11" in phases:
            ffn_phase(xa, xb, 5, *ffn_w[3])
        if "final" in phases:
            final_norm_phase(xb, 6)
        if dbg:
            ds_ = DmaSlot(k, "dbg")
            for nm in ("hT", "actT"):
                dbg.pop(nm, None)
            srcs = {"xa": xa, "xb": xb, "qT": qT, "kT": kT, "uT": uT, "vv": vv, "mixT": mixT, "cpad": cpad, "dpad": dpad, "dbs": dbs}
            t = ds_.issue("sp", [(lambda e, nm=nm: e.dma_start(out=dbg[nm], in_=srcs[nm])) for nm in dbg])
            k.wait("sp", t)
    return nc


build_program.phases = ("ffn00", "projab", "attn", "fnet", "outab", "ffn01", "ffn10", "projcd", "conv", "outcd", "ffn11", "final")


def make_inputs(cfg, inp, b):
    GD = 2
    f32 = np.float32
    m = {"x": np.ascontiguousarray(inp["x"][b], dtype=f32)}
    m["norms"] = np.ascontiguousarray(np.stack([
        inp["ffn1_norm"][0], inp["mix_norm"][0], inp["ffn2_norm"][0],
        inp["ffn1_norm"][1], inp["mix_norm"][1], inp["ffn2_norm"][1], inp["final_norm"]]).astype(f32))
    m["normsT"] = np.ascontiguousarray(m["norms"].reshape(7, cfg.DC, 128).transpose(2, 0, 1))
    for l in range(2):
        for f, pre in enumerate(("ffn1", "ffn2")):
            m[f"wg{l}{f}"] = lay_fm(inp[pre + "_w_gate"][l])
            m[f"wu{l}{f}"] = lay_fm(inp[pre + "_w_up"][l])
            m[f"wd{l}{f}"] = lay_tm(inp[pre + "_w_down"][l], GD)
    w = inp["w_in_ab"][0]
    Q = cfg.QKV
    m["w_ab_fm"] = lay_fm(np.concatenate([w[:, 0:2 * Q], w[:, 3 * Q:]], axis=1))
    m["w_ab_v"] = lay_tm(np.ascontiguousarray(w[:, 2 * Q:3 * Q]), GD)
    m["w_out_ab"] = lay_tm(inp["w_out_ab"][0], GD)
    m["w_cd_fm"] = lay_fm(inp["w_in_cd"][0])
    m["w_out_cd"] = lay_tm(inp["w_out_cd"][0], GD)
    idx, band = bias_index_tables(cfg)
    rb = np.asarray(inp["rel_bias"], dtype=f32)
    bt = np.empty((cfg.HEADS, 128, 3, 256), dtype=f32)
    for g in range(3):
        gathered = rb[idx[g]]
        gathered = np.where(band[:, :, None], gathered, f32(-30000.0))
        bt[:, :, g, :] = gathered.transpose(2, 0, 1)
    m["biasT"] = bt
    m["dft"] = make_inputs.dft
    m["cdft"] = chan_dft(128)
    m["identc"] = np.eye(128, dtype=f32)
    CG, DG = cfg.CC // 128, cfg.CD // 128
    cp = np.concatenate([inp["conv_c_w"][0], inp["conv_c_b"][0][None], inp["ln_c_g"][0][None], inp["ln_c_b"][0][None]], axis=0)
    m["convp"] = np.ascontiguousarray(cp.astype(f32).T.reshape(CG, 128, cfg.CW + 3).transpose(1, 0, 2))
    m["convd"] = np.ascontiguousarray(np.asarray(inp["conv_d_w"][0], dtype=f32).T.reshape(DG, 128, cfg.DW).transpose(1, 0, 2))
    return m


def kernel(**inputs):
    cfg = Cfg()
    inp = {k_: np.asarray(v) for k_, v in inputs.items()}
    B = inp["x"].shape[0]
    make_inputs.dft = dft_tables(cfg.S)
    nc = build_program(cfg)
    in_maps = [make_inputs(cfg, inp, b) for b in range(B)]
    res = run_bass_kernel_spmd(nc, in_maps, core_ids=list(range(B)))
    return np.stack([np.asarray(res.results[b]["y"], dtype=np.float32) for b in range(B)], axis=0)
```
